# Optimizing a Trainium2 kernel written in Bass

```python
import math
import jax, jax.numpy as jnp
from jax import lax
import numpy as np

D_MODEL = 1024
BATCH = 8
SEQ = 4096
DEPTH = 2

HEAD_DIM = 64
DIL_GROUPS = ((128, 1), (512, 4), (2048, 16))
HEADS_PER_GROUP = 4
N_GROUPS = len(DIL_GROUPS)
N_HEADS_A = HEADS_PER_GROUP * N_GROUPS
N_HEADS_B = 4
WIDTH_A = N_HEADS_A * HEAD_DIM
WIDTH_A_OUT = HEADS_PER_GROUP * HEAD_DIM
WIDTH_B = N_HEADS_B * HEAD_DIM
N_BRANCH = 2
SPLIT_SIZES = (WIDTH_A, WIDTH_A, WIDTH_A, WIDTH_B, WIDTH_B, WIDTH_B, N_HEADS_B, D_MODEL, D_MODEL)
SPLIT_POINTS = tuple(int(v) for v in np.cumsum(SPLIT_SIZES)[:-1])
D_IN = int(sum(SPLIT_SIZES))
D_FF = -(-8 * D_MODEL // (3 * 256)) * 256
BLOCK = 128
ALIBI_MAX = 8.0
EPS = 1e-6
N_MOD = 6

kernel_name = "hybrid_dilated_fox_adaln_block"


def rms_norm(x, g):
    xf = x.astype(jnp.float32)
    y = xf * lax.rsqrt(jnp.mean(xf * xf, axis=-1, keepdims=True) + EPS)
    return (y * g.astype(jnp.float32)).astype(x.dtype)


def alibi_slopes():
    h = np.arange(1, N_HEADS_A + 1, dtype=np.float32)
    return jnp.asarray(2.0 ** (-ALIBI_MAX * h / N_HEADS_A), dtype=jnp.float32)


def dilated_group_attention(q, k, v, window, dilation, slopes):
    B, S, H, Dh = q.shape
    L = S // dilation
    nb = -(-L // BLOCK)
    Lp = nb * BLOCK
    pad_end = Lp - L

    def split(t):
        return t.reshape(B, L, dilation, H, Dh).transpose(0, 2, 1, 3, 4)

    def band(t):
        t = jnp.pad(split(t), ((0, 0), (0, 0), (BLOCK, pad_end), (0, 0), (0, 0)))
        t = t.reshape(B, dilation, nb + 1, BLOCK, H, Dh)
        return jnp.concatenate([t[:, :, :-1], t[:, :, 1:]], axis=3)

    qb = jnp.pad(split(q), ((0, 0), (0, 0), (0, pad_end), (0, 0), (0, 0)))
    qb = qb.reshape(B, dilation, nb, BLOCK, H, Dh)
    kb, vb = band(k), band(v)

    scores = jnp.einsum('brnqhd,brnkhd->brnhqk', qb, kb).astype(jnp.float32) * (Dh ** -0.5)
    qi = jnp.arange(BLOCK)[:, None]
    kj = jnp.arange(2 * BLOCK)[None, :]
    dist = qi + BLOCK - kj
    kpos = jnp.arange(nb)[:, None, None] * BLOCK + kj[None] - BLOCK
    valid = (dist >= 0) & (dist <= window // dilation) & (kpos >= 0)
    bias = -slopes[:, None, None] * (dist * dilation).astype(jnp.float32)[None]
    logits = jnp.where(valid[None, None, :, None], scores + bias, -jnp.inf)
    lse = jax.nn.logsumexp(logits, axis=-1)
    p = jnp.exp(logits - lse[..., None])
    out = jnp.einsum('brnhqk,brnkhd->brnqhd', p.astype(v.dtype), vb)
    out = out.reshape(B, dilation, Lp, H, Dh)[:, :, :L].transpose(0, 2, 1, 3, 4).reshape(B, S, H, Dh)
    lse = lse.transpose(0, 1, 2, 4, 3).reshape(B, dilation, Lp, H)[:, :, :L]
    lse = lse.transpose(0, 2, 1, 3).reshape(B, S, H)
    return out, lse


def dilated_mixture(qa, ka, va):
    B, S, _ = qa.shape
    shp = (B, S, N_HEADS_A, HEAD_DIM)
    qa, ka, va = qa.reshape(shp), ka.reshape(shp), va.reshape(shp)
    slopes = alibi_slopes()
    outs, lses = [], []
    for g, (w, d) in enumerate(DIL_GROUPS):
        sl = slice(g * HEADS_PER_GROUP, (g + 1) * HEADS_PER_GROUP)
        o, l = dilated_group_attention(qa[:, :, sl], ka[:, :, sl], va[:, :, sl], w, d, slopes[sl])
        outs.append(o)
        lses.append(l)
    outs = jnp.stack(outs, axis=0)
    alpha = jax.nn.softmax(jnp.stack(lses, axis=0), axis=0)
    y = jnp.sum(alpha[..., None].astype(outs.dtype) * outs, axis=0)
    return y.reshape(B, S, WIDTH_A_OUT)


def forgetting_attention(qb, kb, vb, f_logit, b_forget):
    B, S, _ = qb.shape
    H, Dh = N_HEADS_B, HEAD_DIM
    q = qb.reshape(B, S, H, Dh)
    k = kb.reshape(B, S, H, Dh)
    v = vb.reshape(B, S, H, Dh)
    log_f = jax.nn.log_sigmoid(f_logit.astype(jnp.float32) + b_forget.astype(jnp.float32))
    F = jnp.cumsum(log_f, axis=1).transpose(0, 2, 1)
    nb = S // BLOCK
    q_blocks = q.reshape(B, nb, BLOCK, H, Dh).transpose(1, 0, 2, 3, 4)
    F_blocks = F.reshape(B, H, nb, BLOCK).transpose(2, 0, 1, 3)
    kpos = jnp.arange(S)
    scale = Dh ** -0.5

    def one_block(args):
        i, qi, Fi = args
        s = jnp.einsum('bqhd,bkhd->bhqk', qi, k).astype(jnp.float32) * scale
        s = s + (Fi[..., :, None] - F[..., None, :])
        qpos = i * BLOCK + jnp.arange(BLOCK)
        s = jnp.where(kpos[None, :] <= qpos[:, None], s, -jnp.inf)
        p = jax.nn.softmax(s, axis=-1)
        return jnp.einsum('bhqk,bkhd->bqhd', p.astype(v.dtype), v)

    out = lax.map(one_block, (jnp.arange(nb), q_blocks, F_blocks))
    return out.transpose(1, 0, 2, 3, 4).reshape(B, S, H * Dh)


def token_mixer(h, w_in, b_forget, w_up_a, w_up_b, w_out):
    z = h @ w_in
    qa, ka, va, qb, kb, vb, fz, gza, gzb = jnp.split(z, SPLIT_POINTS, axis=-1)
    ya = dilated_mixture(qa, ka, va) @ w_up_a
    yb = forgetting_attention(qb, kb, vb, fz, b_forget) @ w_up_b
    merged = jax.nn.sigmoid(gza) * ya + jax.nn.sigmoid(gzb) * yb
    return merged @ w_out


def swiglu(h, w_ffn_in, w_ffn_out):
    gate, up = jnp.split(h @ w_ffn_in, 2, axis=-1)
    return (jax.nn.silu(gate) * up) @ w_ffn_out


def setup_inputs(seed: int = 0) -> dict:
    key = jax.random.key(seed)
    ks = jax.random.split(key, 16)
    f32 = jnp.float32
    nrm = lambda k, shape, s: jax.random.normal(k, shape, f32) * s
    x = jax.random.normal(ks[0], (BATCH, SEQ, D_MODEL), f32)
    c = jax.random.normal(ks[1], (BATCH, D_MODEL), f32)
    w_ada = nrm(ks[2], (DEPTH, D_MODEL, N_MOD * D_MODEL), 0.5 * D_MODEL ** -0.5)
    b_ada = nrm(ks[3], (DEPTH, N_MOD * D_MODEL), 0.02)
    norm_mix = 1.0 + nrm(ks[4], (DEPTH, D_MODEL), 0.02)
    w_in = nrm(ks[5], (DEPTH, D_MODEL, D_IN), D_MODEL ** -0.5)
    b_forget = jnp.linspace(1.0, 5.0, N_HEADS_B, dtype=f32)[None] + nrm(ks[6], (DEPTH, N_HEADS_B), 0.1)
    w_up_a = nrm(ks[7], (DEPTH, WIDTH_A_OUT, D_MODEL), WIDTH_A_OUT ** -0.5)
    w_up_b = nrm(ks[8], (DEPTH, WIDTH_B, D_MODEL), WIDTH_B ** -0.5)
    w_out = nrm(ks[9], (DEPTH, D_MODEL, D_MODEL), D_MODEL ** -0.5)
    norm_ffn = 1.0 + nrm(ks[10], (DEPTH, D_MODEL), 0.02)
    w_ffn_in = nrm(ks[11], (DEPTH, D_MODEL, 2 * D_FF), D_MODEL ** -0.5)
    w_ffn_out = nrm(ks[12], (DEPTH, D_FF, D_MODEL), D_FF ** -0.5)
    norm_final = 1.0 + nrm(ks[13], (D_MODEL,), 0.02)
    return {"x": x, "c": c, "w_ada": w_ada, "b_ada": b_ada, "norm_mix": norm_mix,
            "w_in": w_in, "b_forget": b_forget, "w_up_a": w_up_a, "w_up_b": w_up_b,
            "w_out": w_out, "norm_ffn": norm_ffn, "w_ffn_in": w_ffn_in,
            "w_ffn_out": w_ffn_out, "norm_final": norm_final}


def reference(x, c, w_ada, b_ada, norm_mix, w_in, b_forget, w_up_a, w_up_b, w_out,
              norm_ffn, w_ffn_in, w_ffn_out, norm_final):
    c_act = jax.nn.silu(c)
    for l in range(DEPTH):
        mod = c_act @ w_ada[l] + b_ada[l]
        sh1, sc1, g1, sh2, sc2, g2 = [m[:, None, :] for m in jnp.split(mod, N_MOD, axis=-1)]
        h = rms_norm(x, norm_mix[l]) * (1.0 + sc1) + sh1
        x = x + g1 * token_mixer(h, w_in[l], b_forget[l], w_up_a[l], w_up_b[l], w_out[l])
        h = rms_norm(x, norm_ffn[l]) * (1.0 + sc2) + sh2
        x = x + g2 * swiglu(h, w_ffn_in[l], w_ffn_out[l])
    return rms_norm(x, norm_final)
```

```python
import numpy as np
import concourse.bass as bass
import concourse.mybir as mybir
from concourse.bass_utils import run_bass_kernel_spmd

F32 = mybir.dt.float32
BF16 = mybir.dt.bfloat16
I32 = mybir.dt.int32
AF = mybir.ActivationFunctionType
ALU = mybir.AluOpType

D = 1024
S_LEN = 4096
DEPTH = 2
T = 512
NCH = S_LEN // T
D_IN = 5124
D_FF = 2816
NKF = D_FF // 128
GROUPS = ((128, 1), (512, 4), (2048, 16))
OFFS = (1, 4, 16)
RING = 20
EPS = 1e-6
DEBUG = False
STOP = None
P1LEVEL = 9


class _Op:
    __slots__ = ("eng", "fn", "reads", "writes", "dma", "key", "deps", "stream",
                 "pos", "waits", "signals", "vc", "semval", "bar")

    def __init__(self, eng, fn, reads, writes, dma, key):
        self.eng = eng
        self.fn = fn
        self.reads = reads
        self.writes = writes
        self.dma = dma
        self.key = key
        self.signals = False
        self.semval = 0
        self.bar = False


class Sched:
    ENGS = ("pe", "act", "dve", "pool", "sp")
    EPOCH = 29952

    def __init__(self, nc):
        self.nc = nc
        self.ops = []

    def op(self, eng, fn, reads=(), writes=()):
        self.ops.append(_Op(eng, fn, tuple(reads), tuple(writes), False, None))

    def dma(self, eng, fn, reads=(), writes=(), key=None):
        assert key is not None
        self.ops.append(_Op(eng, fn, tuple(reads), tuple(writes), True, key))

    def barrier(self):
        for e in self.ENGS:
            o = _Op(e, None, (), (), False, None)
            o.bar = True
            self.ops.append(o)

    def finalize(self):
        nc = self.nc
        ops = self.ops
        last_w = {}
        readers = {}
        spos = {}
        at = {}
        last_in_stream = {}
        for i, o in enumerate(ops):
            if o.bar:
                o.deps = list(last_in_stream.values())
            else:
                deps = set()
                for r in o.reads:
                    if r in last_w:
                        deps.add(last_w[r])
                for r in o.writes:
                    if r in last_w:
                        deps.add(last_w[r])
                    for j in readers.get(r, ()):
                        deps.add(j)
                deps.discard(i)
                o.deps = [j for j in deps
                          if not (o.eng == "pe" and not o.dma and ops[j].eng == "pe" and not ops[j].dma)]
                for r in o.reads:
                    readers.setdefault(r, []).append(i)
                for r in o.writes:
                    last_w[r] = i
                    readers[r] = []
            o.stream = ("dma", o.key) if o.dma else o.eng
            if o.fn is None:
                o.stream = ("nop", o.eng)
            spos[o.stream] = spos.get(o.stream, 0) + 1
            o.pos = spos[o.stream]
            at[(o.stream, o.pos)] = o
            if o.fn is not None:
                last_in_stream[o.stream] = i
        clock = {e: {} for e in self.ENGS}
        for o in ops:
            need = {}
            for j in o.deps:
                d = ops[j]
                if need.get(d.stream, 0) < d.pos:
                    need[d.stream] = d.pos
            clk = clock[o.eng]
            o.waits = []
            for s, p in need.items():
                if clk.get(s, 0) < p:
                    o.waits.append((s, p))
                    src = at[(s, p)]
                    src.signals = True
                    for s2, p2 in src.vc.items():
                        if clk.get(s2, 0) < p2:
                            clk[s2] = p2
            o.vc = dict(clk)
            o.vc[o.stream] = o.pos
            if o.dma:
                o.signals = True
        cnt = {}
        for o in ops:
            if o.dma:
                o.semval = 16 * o.pos
            elif o.signals:
                cnt[o.stream] = cnt.get(o.stream, 0) + 1
                o.semval = cnt[o.stream]
        sems = {}

        def sem_for(stream, semval, step):
            ep = (semval - step) // self.EPOCH
            k = (stream, ep)
            if k not in sems:
                sems[k] = nc.alloc_semaphore("s_" + str(len(sems)))
            return sems[k], semval - ep * self.EPOCH

        self.n_waits = 0
        block = nc.Block()
        with block:
            def emit(e):
                def body(eng):
                    for o in ops:
                        if o.eng != e:
                            continue
                        for s, p in o.waits:
                            src = at[(s, p)]
                            sm, v = sem_for(s, src.semval, 16 if src.dma else 1)
                            eng.wait_ge(sm, v)
                            self.n_waits += 1
                        if o.fn is None:
                            continue
                        ins = o.fn(eng)
                        if o.signals:
                            step = 16 if o.dma else 1
                            sm, v = sem_for(o.stream, o.semval, step)
                            ins.then_inc(sm, step)
                return body
            block.tensor(emit("pe"))
            block.scalar(emit("act"))
            block.vector(emit("dve"))
            block.gpsimd(emit("pool"))
            block.sync(emit("sp"))
        self.n_sems = len(sems)


class Builder:
    def __init__(self):
        self.nc = nc = bass.Bass("TRN2", target_bir_lowering=False)
        self.S = Sched(nc)
        self.uid = 0
        self.psi = 0
        self.bp = {}
        self.alt = 0
        ei = lambda n, s: nc.dram_tensor(n, s, F32, kind="ExternalInput").ap()
        self.x = ei("x", [S_LEN, D])
        self.c = ei("c", [1, D])
        self.w_ada = ei("w_ada", [DEPTH, D, 6 * D])
        self.b_ada = ei("b_ada", [DEPTH, 6 * D])
        self.norm_mix = ei("norm_mix", [DEPTH, D])
        self.w_in = ei("w_in", [DEPTH, D, D_IN])
        self.b_forget = ei("b_forget", [DEPTH, 4])
        self.w_up_a = ei("w_up_a", [DEPTH, 256, D])
        self.w_up_b = ei("w_up_b", [DEPTH, 256, D])
        self.w_out = ei("w_out", [DEPTH, D, D])
        self.norm_ffn = ei("norm_ffn", [DEPTH, D])
        self.w_ffn_in = ei("w_ffn_in", [DEPTH, D, 2 * D_FF])
        self.w_ffn_out = ei("w_ffn_out", [DEPTH, D_FF, D])
        self.norm_final = ei("norm_final", [1, D])
        self.out = nc.dram_tensor("out", [S_LEN, D], F32, kind="ExternalOutput").ap()
        kind = "ExternalOutput" if DEBUG else "Internal"
        sc = lambda n, s, dt=BF16: nc.dram_tensor(n, s, dt, kind=kind).ap()
        self.xs = sc("xs", [S_LEN, D], F32)
        self.QA = sc("QA", [768, S_LEN])
        self.KA = sc("KA", [768, S_LEN])
        self.VA = sc("VA", [S_LEN, 12 * 65])
        self.QB = sc("QB", [4, 70, S_LEN])
        self.KB = sc("KB", [4, 70, S_LEN])
        self.VB = sc("VB", [S_LEN, 4 * 65])
        self.GZ = sc("GZ", [2048, S_LEN])
        self.OA = sc("OA", [256, S_LEN])
        self.OB = sc("OB", [256, S_LEN])
        self.MS = sc("MS", [128, 4 * 24 * 128])
        self.GS = sc("GS", [2 * DEPTH, 128, D], F32)

    def st(self, name, shape, dt):
        self.uid += 1
        return self.nc.sbuf_tensor("%s_%d" % (name, self.uid), shape, dt)

    def sb(self, name, shape, dt):
        return self.nc.alloc_sbuf_tensor(name, shape, dt)

    def bank(self):
        i = self.psi
        self.psi = (self.psi + 1) % len(self.ps)
        return self.ps[i], ("ps", i)

    def bankp(self, pool):
        k = tuple(pool)
        n = self.bp.get(k, 0)
        self.bp[k] = n + 1
        i = pool[n % len(pool)]
        return self.ps[i], ("ps", i)

    def ev(self):
        self.alt ^= 1
        return "act" if self.alt else "dve"

    def mm(self, out, lhsT, rhs, start, stop, reads, writes):
        self.S.op("pe", lambda g: g.matmul(out, lhsT=lhsT, rhs=rhs, start=start, stop=stop), reads, writes)

    def act(self, out, in_, func, reads, writes, bias=None, scale=1.0, accum=None):
        kw = {}
        if bias is not None:
            kw["bias"] = bias
        if accum is not None:
            kw["accum_out"] = accum
        self.S.op("act", lambda g: g.activation(out=out, in_=in_, func=func, scale=scale, **kw), reads, writes)

    def tt(self, eng, out, in0, in1, op, reads, writes):
        self.S.op(eng, lambda g: g.tensor_tensor(out=out, in0=in0, in1=in1, op=op), reads, writes)

    def ts(self, eng, out, in0, s1, op0, reads, writes, s2=None, op1=None):
        if op1 is None:
            self.S.op(eng, lambda g: g.tensor_scalar(out=out, in0=in0, scalar1=s1, scalar2=None, op0=op0), reads, writes)
        else:
            self.S.op(eng, lambda g: g.tensor_scalar(out=out, in0=in0, scalar1=s1, scalar2=s2, op0=op0, op1=op1), reads, writes)

    def cp(self, eng, out, in_, reads, writes):
        if eng == "act":
            self.S.op("act", lambda g: g.copy(out=out, in_=in_), reads, writes)
        else:
            self.S.op(eng, lambda g: g.tensor_copy(out=out, in_=in_), reads, writes)

    def dma(self, eng, out, in_, reads, writes, key, slow=False):
        if slow:
            self.S.dma(eng, lambda g: g.dma_start(out=out, in_=in_, allow_slow_non_contiguous=True), reads, writes, key)
        else:
            self.S.dma(eng, lambda g: g.dma_start(out=out, in_=in_), reads, writes, key)

    def iota(self, ap, W, writes):
        self.S.op("pool", lambda g: g.iota(ap, pattern=[[1, W]], base=0, channel_multiplier=-1), (), writes)

    def asel(self, ap, pattern, op, fill, base, cm, key):
        self.S.op("pool", lambda g: g.affine_select(out=ap, in_=ap, pattern=pattern, compare_op=op, fill=fill,
                                                    base=base, channel_multiplier=cm), [key], [key])

    def memset(self, eng, ap, v, writes):
        self.S.op(eng, lambda g: g.memset(ap, v), (), writes)

    def consts(self):
        nc = self.nc
        self.ps = [nc.alloc_psum_tensor("ps%d" % i, [128, 512], F32) for i in range(6)]
        self.pT = [nc.alloc_psum_tensor("pT%d" % i, [128, 512], BF16) for i in range(2)]
        self.identb = self.sb("identb", [128, 128], BF16)
        self.onesf = self.sb("onesf", [128, 128], F32)
        self.negtri = self.sb("negtri", [128, 128], F32)
        self.allneg = self.sb("allneg", [128, 128], F32)
        self.cmask = self.sb("cmask", [128, 128], F32)
        self.epst = self.sb("epst", [128, 1], F32)
        self.nfb = self.sb("nfb", [128, D], F32)
        self.pp = [self.sb("pp%d" % l, [128, 32], F32) for l in range(DEPTH)]
        self.bfb = [self.sb("bfb%d" % l, [128, 4], F32) for l in range(DEPTH)]
        self.carry = self.sb("carry", [4, 1], F32)
        self.memset("pool", self.onesf[:], 1.0, ["onesf"])
        self.memset("pool", self.allneg[:], -8.0, ["allneg"])
        self.memset("pool", self.epst[:], EPS, ["epst"])
        with self.st("tmpi", [128, 128], F32) as tmpi:
            self.memset("pool", tmpi[:], 1.0, ["tmpi"])
            self.S.op("pool", lambda g: g.affine_select(out=tmpi[:], in_=tmpi[:], pattern=[[-1, 128]],
                                                        compare_op=ALU.is_equal, fill=0.0, base=0, channel_multiplier=1),
                      ["tmpi"], ["tmpi"])
            self.cp("dve", self.identb[:], tmpi[:], ["tmpi"], ["identb"])
            self.memset("pool", self.negtri[:], -8.0, ["negtri"])
            self.S.op("pool", lambda g: g.affine_select(out=self.negtri[:], in_=self.negtri[:], pattern=[[1, 128]],
                                                        compare_op=ALU.is_ge, fill=0.0, base=0, channel_multiplier=-1),
                      ["negtri"], ["negtri"])
            self.memset("pool", self.cmask[:], 0.0, ["cmask"])
            self.S.op("pool", lambda g: g.affine_select(out=self.cmask[:], in_=self.cmask[:], pattern=[[1, 128]],
                                                        compare_op=ALU.is_ge, fill=-30000.0, base=0, channel_multiplier=-1),
                      ["cmask"], ["cmask"])
            self.dma("sp", self.nfb[:], self.norm_final[0:1, :].to_broadcast([128, D]), [], ["nfb"], "cst")
            for l in range(DEPTH):
                self.dma("sp", self.bfb[l][:], self.b_forget[l:l + 1, :].to_broadcast([128, 4]), [], ["bfb%d" % l], "cst")
            self.S.barrier()
        if STOP == "c0":
            return
        with self.st("crow", [4, 3, 512], BF16) as crow, self.st("crow2", [4, 3, 512], BF16) as crow2:
            self.memset("pool", crow[:], 1.0, ["crow"])
            self.memset("pool", crow2[:], -1.0, ["crow2"])
            for c in range(NCH):
                self.dma("sp", self.KB[:, 64:67, c * T:(c + 1) * T], crow[:], ["crow"], [("KBc", c)], "crow")
                self.dma("sp", self.QB[:, 67:70, c * T:(c + 1) * T], crow2[:], ["crow2"], [("QBc", c)], "crow")
            self.S.barrier()
        if STOP == "c1":
            return
        self.build_masks()
        if STOP == "c2":
            return
        self.build_mod()

    def build_masks(self):
        nc = self.nc
        off0 = 0
        self.moff = []
        for g, (w, d) in enumerate(GROUPS):
            W = (OFFS[g] + 1) * 128
            self.moff.append(off0)
            with self.st("mi", [128, W], I32) as mi, self.st("mf", [128, W], F32) as mf, \
                    self.st("mv", [128, W], F32) as mv, self.st("et", [16, W], F32) as et, \
                    self.st("me", [128, W], F32) as me, self.st("mb", [128, 4, W], BF16) as mb:
                self.iota(mi[:], W, ["mi"])
                self.cp("dve", mf[:], mi[:], ["mi"], ["mf"])
                self.memset("pool", mv[:], 1.0, ["mv"])
                self.asel(mv[:], [[1, W]], ALU.is_ge, 0.0, 0, -1, "mv")
                self.asel(mv[:], [[-1, W]], ALU.is_ge, 0.0, w, 1, "mv")
                if d > 1:
                    self.memset("pool", et[0:d, :], 1.0, ["et"])
                    self.asel(et[0:d, :].rearrange("p (a b) -> p a b", b=d), [[0, W // d], [1, d]], ALU.is_equal, 0.0, 0, -1, "et")
                    for n0 in range(0, W, 512):
                        n1 = min(W, n0 + 512)
                        ps, pk = self.bank()
                        self.mm(ps[:, 0:n1 - n0], et[0:d, 0:128], et[0:d, n0:n1], True, True, ["et"], [pk])
                        self.tt("dve", mv[:, n0:n1], mv[:, n0:n1], ps[:, 0:n1 - n0], ALU.mult, ["mv", pk], ["mv"])
                for i in range(4):
                    slope = 2.0 ** (-8.0 * (g * 4 + i + 1) / 12.0)
                    self.act(me[:], mf[:], AF.Exp, ["mf"], ["me"], scale=-slope)
                    self.tt("dve", mb[:, i, :], me[:], mv[:], ALU.mult, ["me", "mv"], [("mb", i)])
                    self.dma("sp", self.MS[:, (i * 24) * 128 + off0:(i * 24) * 128 + off0 + W], mb[:, i, :],
                             [("mb", i)], [("MS", g, i)], "MS")
                self.S.barrier()
            off0 += W

    def build_mod(self):
        nc = self.nc
        S = self.S
        with self.st("ct", [128, 8], F32) as ct, self.st("crep", [128, 8, 128], F32) as crep, \
                self.st("wa0", [128, 8, 512], F32) as wa0, self.st("wa1", [128, 8, 512], F32) as wa1, \
                self.st("modb", [128, 6 * D], F32) as modb, self.st("badab", [128, 6 * D], F32) as badab, \
                self.st("nmb", [128, D], F32) as nmb, self.st("nfb2", [128, D], F32) as nfb2, \
                self.st("gbb", [128, 2, D], F32) as gbb:
            wa = [wa0, wa1]
            self.dma("sp", ct[:], self.c.rearrange("o (k p) -> p (o k)", p=128), [], ["ct"], "ct", slow=True)
            self.act(ct[:], ct[:], AF.Silu, ["ct"], ["ct"])
            self.cp("dve", crep[:], ct[:, :].unsqueeze(2).to_broadcast([128, 8, 128]), ["ct"], ["crep"])
            nslab = 0
            for l in range(DEPTH):
                self.dma("sp", badab[:], self.b_ada[l:l + 1, :].to_broadcast([128, 6 * D]), [], ["badab"], "badab")
                self.dma("sp", nmb[:], self.norm_mix[l:l + 1, :].to_broadcast([128, D]), [], ["nmb"], "nmb")
                self.dma("sp", nfb2[:], self.norm_ffn[l:l + 1, :].to_broadcast([128, D]), [], ["nfb2"], "nfb2")
                wv = self.w_ada[l].rearrange("(k p) n -> p k n", p=128)
                for n in range(12):
                    wt = wa[nslab % 2]
                    wk = ("wa", nslab % 2)
                    nslab += 1
                    self.dma("sp", wt[:], wv[:, :, n * 512:(n + 1) * 512], [], [wk], wk)
                    ps, pk = self.bank()
                    for k in range(8):
                        self.mm(ps[:], crep[:, k, :], wt[:, k, :], k == 0, k == 7, ["crep", wk], [pk])
                    self.tt("dve", modb[:, n * 512:(n + 1) * 512], ps[:], badab[:, n * 512:(n + 1) * 512], ALU.add,
                            [pk, "badab"], [("modb", n)])
                mk = lambda a: [("modb", a // 512), ("modb", a // 512 + 1)]
                for j, (nb, nk, sco, sho) in enumerate(((nmb, "nmb", D, 0), (nfb2, "nfb2", 4 * D, 3 * D))):
                    self.ts("dve", gbb[:, j, :], modb[:, sco:sco + D], 1.0, ALU.add, mk(sco), [("gbb", j)])
                    self.tt("dve", gbb[:, j, :], gbb[:, j, :], nb[:], ALU.mult, [("gbb", j), nk], [("gbb", j)])
                self.dma("sp", self.GS[2 * l], modb[:, 2 * D:3 * D], mk(2 * D), [("GS", 2 * l)], ("GSk", 2 * l))
                self.dma("sp", self.GS[2 * l + 1], modb[:, 5 * D:6 * D], mk(5 * D), [("GS", 2 * l + 1)], ("GSk", 2 * l + 1))
                ps, pk = self.bank()
                srcs = ((gbb[0:1, 0, :], ("gbb", 0)), (modb[0:1, 0:D], None), (gbb[0:1, 1, :], ("gbb", 1)), (modb[0:1, 3 * D:4 * D], None))
                for v, (src, rk) in enumerate(srcs):
                    rr = [rk] if rk else (mk(0) if v == 1 else mk(3 * D))
                    for k in range(8):
                        self.mm(ps[:, v * 8 + k:v * 8 + k + 1], src[:, k * 128:(k + 1) * 128], self.onesf[0:1, 0:1],
                                True, True, rr + ["onesf"], [pk])
                self.cp("dve", self.pp[l][:], ps[:, 0:32], [pk], ["pp%d" % l])
            S.barrier()

    def norm_hT(self, xt, xk, xn, hT, ppl, col0, ss, rstd, tag):
        self.memset("pool", ss[:], 0.0, [tag + "ss"])
        for t in range(4):
            self.act(xn[:, t, :], xt[:, t, :], AF.Square, [xk, tag + "ss"], [(tag + "xn", t), tag + "ss"], accum=ss[:, t:t + 1])
        self.act(rstd[:], ss[:], AF.Ln, [tag + "ss", "epst"], [tag + "rstd"], bias=self.epst[:], scale=1.0 / D)
        self.act(rstd[:], rstd[:], AF.Exp, [tag + "rstd"], [tag + "rstd"], scale=-0.5)
        for t in range(4):
            self.ts("pool" if t % 2 else "dve", xn[:, t, :], xt[:, t, :], rstd[:, t:t + 1], ALU.mult,
                    [xk, tag + "rstd"], [(tag + "xn", t)])
        for k in range(8):
            pt = self.pT[k % 2]
            pk = ("pT", k % 2)
            for t in range(4):
                self.S.op("pe", lambda g, o=pt[:, t * 128:(t + 1) * 128], i=xn[:, t, k * 128:(k + 1) * 128]:
                          g.transpose(out=o, in_=i, identity=self.identb[:]),
                          [(tag + "xn", t), "identb"], [pk])
            self.ts("dve", hT[:, k, :], pt[:], ppl[:, col0 + k:col0 + k + 1], ALU.mult, [pk, "ppl"], [(tag + "hT", k)],
                    s2=ppl[:, col0 + 8 + k:col0 + 8 + k + 1], op1=ALU.add)

    def phase1(self, l):
        nc = self.nc
        xsrc = self.x if l == 0 else self.xs
        xsk = "xin" if l == 0 else "xs"
        with self.st("wi", [128, 8, D_IN], BF16) as wi, \
                self.st("xt0", [128, 4, D], F32) as xt0, self.st("xt1", [128, 4, D], F32) as xt1, \
                self.st("xn", [128, 4, D], BF16) as xn, self.st("hT", [128, 8, T], BF16) as hT, \
                self.st("zst", [128, 4, T], BF16) as zst, \
                self.st("vsa", [128, 2, 12, 65], BF16) as vsa, self.st("vsb", [128, 2, 4, 65], BF16) as vsb, \
                self.st("ss", [128, 4], F32) as ss, self.st("rstd", [128, 4], F32) as rstd, \
                self.st("fzc", [128, 4, 4], F32) as fzc, self.st("ebn", [128, 4], F32) as ebn, self.st("lnv", [128, 4, 32], F32) as lnv, \
                self.st("fc", [4, T], F32) as fc, self.st("fr", [4, T], F32) as fr, \
                self.st("fa", [4, 3, T], BF16) as fa:
            xts = [xt0, xt1]
            wv = self.w_in[l].rearrange("(k p) n -> p k n", p=128)
            for s in range(4):
                c1 = min(D_IN, (s + 1) * 1536)
                self.dma("pool", wi[:, :, s * 1536:c1], wv[:, :, s * 1536:c1], [], [("wi", s)], ("wi", s))
            wr = lambda a, b: [("wi", s) for s in range(a // 1536, (b - 1) // 1536 + 1)]
            self.memset("dve", vsa[:], 1.0, [("vsa", 0), ("vsa", 1)])
            self.memset("dve", vsb[:], 1.0, [("vsb", 0), ("vsb", 1)])
            self.memset("pool", self.carry[:], 0.0, ["carry"])
            self.memset("pool", lnv[:], 0.0, ["lnv"])
            self.act(ebn[:], self.bfb[l][:], AF.Exp, ["bfb%d" % l], ["ebn"], scale=-1.0)
            ld = lambda c: self.dma("sp", xts[c % 2][:], xsrc[c * T:(c + 1) * T, :].rearrange("(t p) d -> p t d", p=128),
                                    [(xsk, c)], [("xt", c % 2)], ("xt", c % 2))
            ld(0)
            nz = 0
            nv = 0
            for c in range(NCH):
                if c + 1 < NCH:
                    ld(c + 1)
                if P1LEVEL < 1:
                    continue
                self.norm_hT(xts[c % 2], ("xt", c % 2), xn, hT, self.pp[l], 0, ss, rstd, "p1")
                hk = [("p1hT", k) for k in range(8)]
                if P1LEVEL < 2:
                    continue
                tok = slice(c * T, (c + 1) * T)
                tiles = []
                for m in range(6):
                    tiles.append((m * 128, [(self.QA[m * 128:(m + 1) * 128, tok], slice(0, 128), ("QA", c, m))]))
                for m in range(6):
                    tiles.append((768 + m * 128, [(self.KA[m * 128:(m + 1) * 128, tok], slice(0, 128), ("KA", c, m))]))
                for m in range(2):
                    tiles.append((2304 + m * 128, [(self.QB[2 * m, 0:64, tok], slice(0, 64), ("QB", c, 2 * m)),
                                                   (self.QB[2 * m + 1, 0:64, tok], slice(64, 128), ("QB", c, 2 * m + 1))]))
                for m in range(2):
                    tiles.append((2560 + m * 128, [(self.KB[2 * m, 0:64, tok], slice(0, 64), ("KB", c, 2 * m)),
                                                   (self.KB[2 * m + 1, 0:64, tok], slice(64, 128), ("KB", c, 2 * m + 1))]))
                for m in range(8):
                    tiles.append((3076 + m * 128, [(self.GZ[m * 128:(m + 1) * 128, tok], slice(0, 128), ("GZ", c, m))]))
                for m in range(8):
                    tiles.append((4100 + m * 128, [(self.GZ[1024 + m * 128:1024 + (m + 1) * 128, tok], slice(0, 128), ("GZ", c, 8 + m))]))
                for col0, dsts in tiles:
                    ps, pk = self.bank()
                    for k in range(8):
                        self.mm(ps[:], wi[:, k, col0:col0 + 128], hT[:, k, :], k == 0, k == 7,
                                wr(col0, col0 + 128) + [hk[k]], [pk])
                    zi = nz % 4
                    nz += 1
                    self.cp(self.ev(), zst[:, zi, :], ps[:], [pk], [("zst", zi)])
                    for di, (dst, rows, dk) in enumerate(dsts):
                        self.dma("sp", dst, zst[rows, zi, :], [("zst", zi)], [dk], ("zsto", zi, di))
                for t in range(4 if P1LEVEL >= 3 else 0):
                    vi = nv % 2
                    nv += 1
                    t0 = c * T + t * 128
                    groups = ((1536, 512), (2048, 256), (2816, 260))
                    pss = []
                    for (c0, n) in groups:
                        ps, pk = self.bank()
                        for k in range(8):
                            self.mm(ps[:, 0:n], hT[:, k, t * 128:(t + 1) * 128], wi[:, k, c0:c0 + n], k == 0, k == 7,
                                    wr(c0, c0 + n) + [hk[k]], [pk])
                        pss.append((ps, pk))
                    self.cp("act", vsa[:, vi, 0:8, 0:64], pss[0][0][:, 0:512].rearrange("p (h e) -> p h e", e=64),
                            [pss[0][1]], [("vsa", vi)])
                    self.cp("dve", vsa[:, vi, 8:12, 0:64], pss[1][0][:, 0:256].rearrange("p (h e) -> p h e", e=64),
                            [pss[1][1]], [("vsa", vi)])
                    self.cp("act", vsb[:, vi, :, 0:64], pss[2][0][:, 0:256].rearrange("p (h e) -> p h e", e=64),
                            [pss[2][1]], [("vsb", vi)])
                    self.dma("sp", self.VA[t0:t0 + 128, :], vsa[:, vi, :, :].rearrange("p h e -> p (h e)"),
                             [("vsa", vi)], [("VA", c, t)], ("VA", vi))
                    self.dma("sp", self.VB[t0:t0 + 128, :], vsb[:, vi, :, :].rearrange("p h e -> p (h e)"),
                             [("vsb", vi)], [("VB", c, t)], ("VB", vi))
                    if P1LEVEL < 4:
                        continue
                    self.act(fzc[:, t, :], pss[2][0][:, 256:260], AF.Exp, [pss[2][1]], [("fzc", t)], scale=-1.0)
                if P1LEVEL < 4.1:
                    continue
                fzk = [("fzc", t) for t in range(4)]
                self.tt("dve", lnv[:, :, 0:4], fzc[:, :, :], ebn[:, :].unsqueeze(1).to_broadcast([128, 4, 4]), ALU.mult,
                        fzk + ["ebn"], ["lnv"])
                self.act(lnv[:, :, 0:4], lnv[:, :, 0:4], AF.Ln, ["lnv", "onesf"], ["lnv"], bias=self.onesf[:, 0:1])
                if P1LEVEL < 4.2:
                    continue
                psF, pkF = self.bank()
                for t in range(4):
                    for t2 in range(t):
                        self.mm(psF[0:32, t * 128:(t + 1) * 128], lnv[:, t2, :], self.allneg[:], t2 == 0, False, ["lnv", "allneg"], [pkF])
                    self.mm(psF[0:32, t * 128:(t + 1) * 128], lnv[:, t, :], self.negtri[:], t == 0, True, ["lnv", "negtri"], [pkF])
                self.ts("dve", fc[:], psF[0:4, 0:512], self.carry[:, 0:1], ALU.add, [pkF, "carry"], ["fc"])
                self.cp("dve", self.carry[:], fc[:, 511:512], ["fc"], ["carry"])
                if P1LEVEL < 5:
                    continue
                self.cp("dve", fa[:, 0, :], fc[:], ["fc"], [("fa", 0)])
                self.tt("dve", fr[:], fc[:], fa[:, 0, :], ALU.subtract, ["fc", ("fa", 0)], ["fr"])
                self.cp("dve", fa[:, 1, :], fr[:], ["fr"], [("fa", 1)])
                self.tt("dve", fr[:], fr[:], fa[:, 1, :], ALU.subtract, ["fr", ("fa", 1)], ["fr"])
                self.cp("dve", fa[:, 2, :], fr[:], ["fr"], [("fa", 2)])
                far = [("fa", 0), ("fa", 1), ("fa", 2)]
                if P1LEVEL < 6:
                    continue
                self.dma("sp", self.QB[:, 64:67, tok], fa[:], far, [("QBf", c)], ("QBf", 0))
                self.dma("sp", self.KB[:, 67:70, tok], fa[:], far, [("KBf", c)], ("KBf", 0))
            self.S.barrier()

    def phase2(self, l):
        nc = self.nc
        with self.st("KAr", [128, 6, RING * 128], BF16) as KAr, self.st("VAr", [128, RING, 780], BF16) as VAr, \
                self.st("KBr", [70, 4, S_LEN], BF16) as KBr, self.st("VBr", [128, 32, 260], BF16) as VBr, \
                self.st("MSr", [128, 4 * 24 * 128], BF16) as MSr, \
                self.st("QAc", [128, 2, 6, T], BF16) as QAc, self.st("QBc", [70, 2, 4, T], BF16) as QBc, \
                self.st("pTs", [128, 4, T], BF16) as pTs, self.st("dtmp", [128, 2, 128], F32) as dtmp, \
                self.st("rr", [128, 2, T], F32) as rr, self.st("bcs", [64, 2, T], F32) as bcs, \
                self.st("ost", [64, 4, T], BF16) as ost:
            self.dma("sp", MSr[:], self.MS[:, :], [("MS", g, i) for g in range(3) for i in range(4)], ["MSr"], "MSr")
            npt = 0
            nh = 0
            for c in range(NCH):
                tok = slice(c * T, (c + 1) * T)
                qi = c % 2
                self.dma("sp", QAc[:, qi, :, :], self.QA.rearrange("(m p) t -> p m t", p=128)[:, :, tok],
                         [("QA", c, m) for m in range(6)], [("QAc", qi)], ("QAc", qi))
                self.dma("sp", QBc[:, qi, :, :], self.QB.rearrange("h r t -> r h t")[:, :, tok],
                         [("QB", c, h) for h in range(4)] + [("QBc", c)] + [("QBf", c)], [("QBc_", qi)], ("QBc_", qi))
                rs = (c % 5) * 4
                self.dma("sp", KAr[:, :, rs * 128:(rs + 4) * 128], self.KA.rearrange("(m p) t -> p m t", p=128)[:, :, tok],
                         [("KA", c, m) for m in range(6)], [("KAr", c % 5)], ("KAr", c % 5))
                self.dma("sp", VAr[:, rs:rs + 4, :], self.VA[tok, :].rearrange("(j p) f -> p j f", p=128),
                         [("VA", c, t) for t in range(4)], [("VAr", c % 5)], ("VAr", c % 5))
                self.dma("sp", KBr[:, :, tok], self.KB.rearrange("h r t -> r h t")[:, :, tok],
                         [("KB", c, h) for h in range(4)] + [("KBc", c)] + [("KBf", c)], [("KBr", c), ("KBring", c % 2)], ("KBr", c % 2))
                self.dma("sp", VBr[:, 4 * c:4 * c + 4, :], self.VB[tok, :].rearrange("(j p) f -> p j f", p=128),
                         [("VB", c, t) for t in range(4)], [("VBr", c), ("VBring", c % 2)], ("VBr", c % 2))
                for hd in range(8):
                    po, pok = self.bankp([0, 1])
                    first = True
                    if hd < 4:
                        i = hd
                        pb0 = (i % 2) * 64
                        items = []
                        for g in (2, 1, 0):
                            for kb in range(4 * c + 3, max(0, 4 * c - OFFS[g]) - 1, -1):
                                items.append((g, kb))
                        items.remove((2, 4 * c))
                        items.insert(0, (2, 4 * c))
                        for ii, (g, kb) in enumerate(items):
                            m = g * 2 + i // 2
                            qlo = max(kb, 4 * c) - 4 * c
                            qhi = min(kb + OFFS[g], 4 * c + 3) - 4 * c + 1
                            cs = slice(qlo * 128, qhi * 128)
                            slot = kb % RING
                            ps, pk = self.bankp([2, 3, 4])
                            self.mm(ps[:, cs], KAr[pb0:pb0 + 64, m, slot * 128:(slot + 1) * 128], QAc[pb0:pb0 + 64, qi, m, cs],
                                    True, True, [("KAr", (kb // 4) % 5), ("QAc", qi)], [pk])
                            pi = npt % 4
                            npt += 1
                            self.act(pTs[:, pi, cs], ps[:, cs], AF.Exp, [pk], [("pTs", pi)], scale=0.125)
                            o_lo = 4 * c + qlo - kb
                            mo = (i * 24) * 128 + self.moff[g] + o_lo * 128
                            self.tt("pool" if npt % 2 else "dve", pTs[:, pi, cs], pTs[:, pi, cs], MSr[:, mo:mo + (qhi - qlo) * 128], ALU.mult,
                                    [("pTs", pi), "MSr"], [("pTs", pi)])
                            vc0 = (g * 4 + i) * 65
                            self.mm(po[0:65, cs], VAr[:, slot, vc0:vc0 + 65], pTs[:, pi, cs], first, ii == len(items) - 1,
                                    [("VAr", (kb // 4) % 5), ("pTs", pi)], [pok])
                            first = False
                    else:
                        h = hd - 4
                        for kb in range(0, 4 * c + 4):
                            jj = kb - 4 * c
                            qlo = max(jj, 0)
                            cs = slice(qlo * 128, T)
                            ps, pk = self.bankp([2, 3, 4])
                            self.mm(ps[:, cs], KBr[:, h, kb * 128:(kb + 1) * 128], QBc[:, qi, h, cs], True, True,
                                    [("KBr", kb // 4), ("QBc_", qi)] + ([("KBring", c % 2)] if kb // 4 == c else []), [pk])
                            pi = npt % 4
                            npt += 1
                            if jj >= 0:
                                di = npt % 2
                                dsl = slice(jj * 128, (jj + 1) * 128)
                                self.tt("dve", dtmp[:, di, :], ps[:, dsl], self.cmask[:], ALU.add, [pk, "cmask"], [("dtmp", di)])
                                self.act(pTs[:, pi, dsl], dtmp[:, di, :], AF.Exp, [("dtmp", di)], [("pTs", pi)], scale=0.125)
                                if jj < 3:
                                    rs2 = slice((jj + 1) * 128, T)
                                    self.act(pTs[:, pi, rs2], ps[:, rs2], AF.Exp, [pk], [("pTs", pi)], scale=0.125)
                            else:
                                self.act(pTs[:, pi, cs], ps[:, cs], AF.Exp, [pk], [("pTs", pi)], scale=0.125)
                            self.mm(po[0:65, cs], VBr[:, kb, h * 65:(h + 1) * 65], pTs[:, pi, cs], first, kb == 4 * c + 3,
                                    [("VBr", kb // 4), ("pTs", pi)] + ([("VBring", c % 2)] if kb // 4 == c else []), [pok])
                            first = False
                    ri = nh % 2
                    oi = nh % 4
                    nh += 1
                    self.S.op("dve", lambda g_, o=rr[64:65, ri, :], i_=po[64:65, :]: g_.reciprocal(out=o, in_=i_), [pok], [("rr", ri)])
                    pb, pbk = self.bankp([5])
                    self.mm(pb[0:64, :], self.onesf[64:65, 0:64], rr[64:65, ri, :], True, True, [("rr", ri), "onesf"], [pbk])
                    self.cp("act", bcs[:, ri, :], pb[0:64, :], [pbk], [("bcs", ri)])
                    self.tt("dve", ost[:, oi, :], po[0:64, :], bcs[:, ri, :], ALU.mult, [pok, ("bcs", ri)], [("ost", oi)])
                    dst = self.OA if hd < 4 else self.OB
                    hh = hd % 4
                    self.dma("sp", dst[hh * 64:(hh + 1) * 64, tok], ost[:, oi, :], [("ost", oi)],
                             [("OA" if hd < 4 else "OB", c, hh)], ("O", oi))
            self.S.barrier()

    def phase3a(self, l):
        nc = self.nc
        xsrc = self.x if l == 0 else self.xs
        xsk = "xin" if l == 0 else "xs"
        with self.st("wua", [128, 2, D], BF16) as wua, self.st("wub", [128, 2, D], BF16) as wub, \
                self.st("wo", [128, 8, D], BF16) as wo, \
                self.st("xt0", [128, 4, D], F32) as xt0, self.st("xt1", [128, 4, D], F32) as xt1, \
                self.st("oc", [128, 2, 4, T], BF16) as oc, self.st("gz", [128, 2, 16, T], BF16) as gz, \
                self.st("mg", [128, 8, T], BF16) as mg, self.st("sg", [128, 4, T], F32) as sg, \
                self.st("m1", [128, 4, T], F32) as m1, self.st("tmp", [128, 2, T], F32) as tmp, \
                self.st("gl", [128, D], F32) as gl:
            xts = [xt0, xt1]
            self.dma("sp", gl[:], self.GS[2 * l], [("GS", 2 * l)], ["gl"], "gl")
            self.dma("pool", wua[:], self.w_up_a[l].rearrange("(k p) n -> p k n", p=128), [], ["wua"], "wua")
            self.dma("pool", wub[:], self.w_up_b[l].rearrange("(k p) n -> p k n", p=128), [], ["wub"], "wub")
            wov = self.w_out[l].rearrange("(k p) n -> p k n", p=128)
            for n in range(2):
                self.dma("pool", wo[:, :, n * 512:(n + 1) * 512], wov[:, :, n * 512:(n + 1) * 512], [], [("wo", n)], ("wo", n))

            def ld(c):
                b = c % 2
                tok = slice(c * T, (c + 1) * T)
                self.dma("sp", xts[b][:], xsrc[tok, :].rearrange("(t p) d -> p t d", p=128), [(xsk, c)], [("xt", b)], ("xt", b))
                self.dma("sp", oc[:, b, 0:2, :], self.OA.rearrange("(m p) t -> p m t", p=128)[:, :, tok],
                         [("OA", c, h) for h in range(4)], [("oca", b)], ("oca", b))
                self.dma("sp", oc[:, b, 2:4, :], self.OB.rearrange("(m p) t -> p m t", p=128)[:, :, tok],
                         [("OB", c, h) for h in range(4)], [("ocb", b)], ("ocb", b))
                self.dma("sp", gz[:, b, :, :], self.GZ.rearrange("(m p) t -> p m t", p=128)[:, :, tok],
                         [("GZ", c, m) for m in range(16)], [("gz", b)], ("gz", b))
            ld(0)
            ns = 0
            for c in range(NCH):
                b = c % 2
                if c + 1 < NCH:
                    ld(c + 1)
                for j in range(8):
                    si = (ns % 2) * 2
                    ns += 1
                    psA, pkA = self.bank()
                    for k in range(2):
                        self.mm(psA[:], wua[:, k, j * 128:(j + 1) * 128], oc[:, b, k, :], k == 0, k == 1, ["wua", ("oca", b)], [pkA])
                    psB, pkB = self.bank()
                    for k in range(2):
                        self.mm(psB[:], wub[:, k, j * 128:(j + 1) * 128], oc[:, b, 2 + k, :], k == 0, k == 1, ["wub", ("ocb", b)], [pkB])
                    self.act(sg[:, si, :], gz[:, b, j, :], AF.Sigmoid, [("gz", b)], [("sg", si)])
                    self.act(sg[:, si + 1, :], gz[:, b, 8 + j, :], AF.Sigmoid, [("gz", b)], [("sg", si + 1)])
                    self.tt("dve", m1[:, si, :], psA[:], sg[:, si, :], ALU.mult, [pkA, ("sg", si)], [("m1", si)])
                    self.tt("dve", m1[:, si + 1, :], psB[:], sg[:, si + 1, :], ALU.mult, [pkB, ("sg", si + 1)], [("m1", si + 1)])
                    self.tt("pool", mg[:, j, :], m1[:, si, :], m1[:, si + 1, :], ALU.add, [("m1", si), ("m1", si + 1)], [("mg", j)])
                nt = 0
                for t in range(4):
                    for n in range(2):
                        ps, pk = self.bank()
                        for k in range(8):
                            self.mm(ps[:], mg[:, k, t * 128:(t + 1) * 128], wo[:, k, n * 512:(n + 1) * 512], k == 0, k == 7,
                                    [("mg", k), ("wo", n)], [pk])
                        ti = nt % 2
                        nt += 1
                        self.tt("dve", tmp[:, ti, :], ps[:], gl[:, n * 512:(n + 1) * 512], ALU.mult,
                                [pk, "gl"], [("tmp", ti)])
                        xsl = xts[b][:, t, n * 512:(n + 1) * 512]
                        self.tt("pool", xsl, xsl, tmp[:, ti, :], ALU.add, [("xt", b), ("tmp", ti)], [("xt", b)])
                self.dma("sp", self.xs[c * T:(c + 1) * T, :].rearrange("(t p) d -> p t d", p=128), xts[b][:],
                         [("xt", b)], [("xs", c)], ("xso", b))
            self.S.barrier()

    def phase3b(self, l):
        nc = self.nc
        last = l == DEPTH - 1
        with self.st("wf", [128, 8, 2 * D_FF], BF16) as wf, self.st("wfo", [128, 2, 11, 512], BF16) as wfo, \
                self.st("xt0", [128, 4, D], F32) as xt0, self.st("xt1", [128, 4, D], F32) as xt1, \
                self.st("xn", [128, 4, D], BF16) as xn, self.st("hT", [128, 8, T], BF16) as hT, \
                self.st("aT", [128, NKF, T], BF16) as aT, self.st("sgl", [128, 2, T], BF16) as sgl, \
                self.st("tmp", [128, 2, T], F32) as tmp, \
                self.st("ss", [128, 4], F32) as ss, self.st("rstd", [128, 4], F32) as rstd, \
                self.st("gl", [128, D], F32) as gl:
            xts = [xt0, xt1]
            self.dma("sp", gl[:], self.GS[2 * l + 1], [("GS", 2 * l + 1)], ["gl"], "gl")
            wv = self.w_ffn_in[l].rearrange("(k p) n -> p k n", p=128)
            for s in (0, 2, 1, 3):
                self.dma("pool", wf[:, :, s * 1408:(s + 1) * 1408], wv[:, :, s * 1408:(s + 1) * 1408], [], [("wf", s)], ("wf", s))
            wr = lambda a, b: [("wf", s) for s in range(a // 1408, (b - 1) // 1408 + 1)]
            wov = self.w_ffn_out[l].rearrange("(k p) n -> p k n", p=128)
            ld = lambda c: self.dma("sp", xts[c % 2][:], self.xs[c * T:(c + 1) * T, :].rearrange("(t p) d -> p t d", p=128),
                                    [("xs", c)], [("xt", c % 2)], ("xt", c % 2))
            ld(0)
            nsl = 0
            ng = 0
            for c in range(NCH):
                b = c % 2
                xt = xts[b]
                if c + 1 < NCH:
                    ld(c + 1)
                self.norm_hT(xt, ("xt", b), xn, hT, self.pp[l], 16, ss, rstd, "p3")
                hk = [("p3hT", k) for k in range(8)]
                for f in range(NKF):
                    psG, pkG = self.bank()
                    for k in range(8):
                        self.mm(psG[:], wf[:, k, f * 128:(f + 1) * 128], hT[:, k, :], k == 0, k == 7,
                                wr(f * 128, (f + 1) * 128) + [hk[k]], [pkG])
                    psU, pkU = self.bank()
                    for k in range(8):
                        self.mm(psU[:], wf[:, k, D_FF + f * 128:D_FF + (f + 1) * 128], hT[:, k, :], k == 0, k == 7,
                                wr(D_FF + f * 128, D_FF + (f + 1) * 128) + [hk[k]], [pkU])
                    gi = ng % 2
                    ng += 1
                    self.act(sgl[:, gi, :], psG[:], AF.Silu, [pkG], [("sgl", gi)])
                    self.tt("dve", aT[:, f, :], psU[:], sgl[:, gi, :], ALU.mult, [pkU, ("sgl", gi)], [("aT", f)])
                for n in range(2):
                    accs = [self.bank() for _ in range(4)]
                    for kh in range(2):
                        wi_ = nsl % 2
                        nsl += 1
                        wk = ("wfo", wi_)
                        self.dma("pool", wfo[:, wi_, :, :], wov[:, kh * 11:(kh + 1) * 11, n * 512:(n + 1) * 512], [], [wk], wk)
                        for t in range(4):
                            ps, pk = accs[t]
                            for k in range(11):
                                self.mm(ps[:], aT[:, kh * 11 + k, t * 128:(t + 1) * 128], wfo[:, wi_, k, :],
                                        kh == 0 and k == 0, kh == 1 and k == 10, [("aT", kh * 11 + k), wk], [pk])
                    for t in range(4):
                        ps, pk = accs[t]
                        ti = t % 2
                        self.tt("dve", tmp[:, ti, :], ps[:], gl[:, n * 512:(n + 1) * 512], ALU.mult,
                                [pk, "gl"], [("tmp", ti)])
                        xsl = xt[:, t, n * 512:(n + 1) * 512]
                        self.tt("pool", xsl, xsl, tmp[:, ti, :], ALU.add, [("xt", b), ("tmp", ti)], [("xt", b)])
                if not last:
                    self.dma("sp", self.xs[c * T:(c + 1) * T, :].rearrange("(t p) d -> p t d", p=128), xt[:],
                             [("xt", b)], [("xs", c)], ("xso", b))
                else:
                    self.memset("pool", ss[:], 0.0, ["p3ss"])
                    for t in range(4):
                        self.act(xn[:, t, :], xt[:, t, :], AF.Square, [("xt", b), "p3ss"], [("p3xn", t), "p3ss"], accum=ss[:, t:t + 1])
                    self.act(rstd[:], ss[:], AF.Ln, ["p3ss", "epst"], ["p3rstd"], bias=self.epst[:], scale=1.0 / D)
                    self.act(rstd[:], rstd[:], AF.Exp, ["p3rstd"], ["p3rstd"], scale=-0.5)
                    for t in range(4):
                        self.S.op("dve", lambda g, o=xt[:, t, :], r=rstd[:, t:t + 1]:
                                  g.scalar_tensor_tensor(out=o, in0=o, scalar=r, in1=self.nfb[:], op0=ALU.mult, op1=ALU.mult),
                                  [("xt", b), "p3rstd", "nfb"], [("xt", b)])
                    self.dma("sp", self.out[c * T:(c + 1) * T, :].rearrange("(t p) d -> p t d", p=128), xt[:],
                             [("xt", b)], [("out", c)], ("outo", b))
            self.S.barrier()

    def build(self):
        self.consts()
        for l in range(DEPTH):
            if STOP in ("c0", "c1", "c2", "c3"):
                break
            self.phase1(l)
            if STOP == "p1":
                break
            self.phase2(l)
            if STOP == "p2":
                break
            self.phase3a(l)
            if STOP == "p3a":
                break
            self.phase3b(l)
        self.S.barrier()
        self.S.op("sp", None, [("out", c) for c in range(NCH)], [])
        self.S.finalize()
        return self.nc


_NC_CACHE = {}


def kernel(x, c, w_ada, b_ada, norm_mix, w_in, b_forget, w_up_a, w_up_b, w_out,
           norm_ffn, w_ffn_in, w_ffn_out, norm_final):
    f = lambda a: np.ascontiguousarray(np.asarray(a, dtype=np.float32))
    if "nc" not in _NC_CACHE:
        _NC_CACHE["nc"] = Builder().build()
    nc = _NC_CACHE["nc"]
    shared = {"w_ada": f(w_ada), "b_ada": f(b_ada), "norm_mix": f(norm_mix), "w_in": f(w_in),
              "b_forget": f(b_forget), "w_up_a": f(w_up_a), "w_up_b": f(w_up_b), "w_out": f(w_out),
              "norm_ffn": f(norm_ffn), "w_ffn_in": f(w_ffn_in), "w_ffn_out": f(w_ffn_out),
              "norm_final": f(norm_final).reshape(1, D)}
    x = f(x)
    c = f(c)
    in_maps = []
    for b in range(8):
        m = dict(shared)
        m["x"] = x[b]
        m["c"] = c[b].reshape(1, D)
        in_maps.append(m)
    res = run_bass_kernel_spmd(nc, in_maps, core_ids=list(range(8)))
    return np.stack([np.asarray(r["out"], dtype=np.float32) for r in res.results], axis=0)
```
